# Optimizing a Trainium2 kernel written in Bass

```python
import math
import jax, jax.numpy as jnp
from jax import lax
import numpy as np

D_MODEL = 2048
BATCH = 2
SEQ = 8192
DEPTH = 4

N_MIXERS = 3
N_LRU_LAYERS = (DEPTH + N_MIXERS - 1) // N_MIXERS
N_NSA_LAYERS = (DEPTH - 1 + N_MIXERS - 1) // N_MIXERS
N_RET_LAYERS = (DEPTH - 2 + N_MIXERS - 1) // N_MIXERS

DN_ALPHA = (2 * DEPTH) ** 0.25
DN_BETA = (8 * DEPTH) ** -0.25
LN_EPS = 1e-5
GN_EPS = 1e-5

D_FF = 4 * D_MODEL

LRU_WIDTH = D_MODEL
LRU_BLOCKS = 8
LRU_BLOCK_DIM = LRU_WIDTH // LRU_BLOCKS
CONV_WIDTH = 4
LRU_C = 8.0

NSA_HEADS = 16
NSA_HEAD_DIM = 128
NSA_KV_GROUPS = 4
NSA_HEADS_PER_GROUP = NSA_HEADS // NSA_KV_GROUPS
NSA_Q_WIDTH = NSA_HEADS * NSA_HEAD_DIM
NSA_KV_WIDTH = NSA_KV_GROUPS * NSA_HEAD_DIM
N_BRANCH = 3
NSA_IN_WIDTH = NSA_Q_WIDTH + 2 * N_BRANCH * NSA_KV_WIDTH + N_BRANCH * NSA_HEADS
NSA_SPLITS = tuple(NSA_Q_WIDTH + i * NSA_KV_WIDTH for i in range(2 * N_BRANCH + 1))
CMP_BLOCK = 32
CMP_STRIDE = 16
SEL_BLOCK = 64
SEL_RATIO = SEL_BLOCK // CMP_STRIDE
SEL_TOP_N = 16
SEL_FORCE = 1e6
WINDOW = 512
NSA_Q_BLOCK = 64
NEG_BIG = 1e30

T5_BUCKETS = 32
T5_MAX_DIST = 1024

RET_HEADS = 8
RET_QK_DIM = D_MODEL // RET_HEADS
RET_V_DIM = 2 * RET_QK_DIM
RET_QK_TOTAL = RET_HEADS * RET_QK_DIM
RET_V_TOTAL = RET_HEADS * RET_V_DIM
RET_IN_WIDTH = 2 * RET_QK_TOTAL + 2 * RET_V_TOTAL
RET_SPLITS = (RET_QK_TOTAL, 2 * RET_QK_TOTAL, 2 * RET_QK_TOTAL + RET_V_TOTAL)
RET_CHUNK = 128
ROPE_BASE = 10000.0

kernel_name = 'hybrid_rglru_nsa_retention_block'


def _layer_norm(x, w, b):
    xf = x.astype(jnp.float32)
    mu = xf.mean(-1, keepdims=True)
    var = jnp.mean(jnp.square(xf - mu), -1, keepdims=True)
    return ((xf - mu) * lax.rsqrt(var + LN_EPS) * w + b).astype(x.dtype)


def _masked_softmax(logits, mask):
    logits = logits.astype(jnp.float32)
    m = jnp.max(jnp.where(mask, logits, -NEG_BIG), axis=-1, keepdims=True)
    e = jnp.exp(jnp.where(mask, logits - m, -jnp.inf))
    return e / jnp.maximum(e.sum(-1, keepdims=True), 1e-30)


def _t5_bucket(dist):
    n = jnp.maximum(dist, 0)
    max_exact = T5_BUCKETS // 2
    log_ratio = jnp.log(jnp.maximum(n, 1).astype(jnp.float32) / max_exact) / math.log(T5_MAX_DIST / max_exact)
    large = jnp.minimum(max_exact + (log_ratio * (T5_BUCKETS - max_exact)).astype(jnp.int32), T5_BUCKETS - 1)
    return jnp.where(n < max_exact, n, large)


def _rotary(x, pos):
    half = x.shape[-1] // 2
    inv = ROPE_BASE ** (-jnp.arange(half, dtype=jnp.float32) / half)
    ang = pos.astype(jnp.float32)[:, None] * inv[None, :]
    cos = jnp.cos(ang)[None, :, None, :].astype(x.dtype)
    sin = jnp.sin(ang)[None, :, None, :].astype(x.dtype)
    x1, x2 = x[..., :half], x[..., half:]
    return jnp.concatenate([x1 * cos - x2 * sin, x1 * sin + x2 * cos], axis=-1)


def _linear_combine(lhs, rhs):
    a_l, b_l = lhs
    a_r, b_r = rhs
    return a_l * a_r, a_r * b_l + b_r


def rglru_mixer(h, w_in, conv_w, conv_b, gate_w, gate_b, lam, w_out):
    bsz, seq, _ = h.shape
    f32 = jnp.float32
    y_br, x_br = jnp.split(h @ w_in, 2, axis=-1)
    xp = jnp.pad(x_br, ((0, 0), (CONV_WIDTH - 1, 0), (0, 0)))
    xc = conv_b + sum(xp[:, k:k + seq] * conv_w[k] for k in range(CONV_WIDTH))
    xb = xc.reshape(bsz, seq, LRU_BLOCKS, LRU_BLOCK_DIM)
    gates = jnp.einsum('bsnd,jnde->jbsne', xb, gate_w).reshape(2, bsz, seq, LRU_WIDTH) + gate_b[:, None, None, :]
    r = jax.nn.sigmoid(gates[0].astype(f32))
    i = jax.nn.sigmoid(gates[1].astype(f32))
    log_a = -LRU_C * r * jax.nn.softplus(-lam.astype(f32))
    a = jnp.exp(log_a)
    u = jnp.sqrt(-jnp.expm1(2.0 * log_a)) * (i * xc.astype(f32))
    _, hs = lax.associative_scan(_linear_combine, (a, u), axis=1)
    return ((jax.nn.gelu(y_br) * hs.astype(h.dtype)) @ w_out).astype(h.dtype)


def nsa_mixer(h, rel_bias, w_in, cmp_pe, cmp_w1, cmp_b1, cmp_w2, w_out):
    bsz, seq, _ = h.shape
    f32 = jnp.float32
    G, HPG, DH, QB = NSA_KV_GROUPS, NSA_HEADS_PER_GROUP, NSA_HEAD_DIM, NSA_Q_BLOCK
    q, kc_tok, vc_tok, ks_tok, vs_tok, kw_tok, vw_tok, g = jnp.split(h @ w_in, list(NSA_SPLITS), axis=-1)
    q = (q * DH ** -0.5).reshape(bsz, seq, G, HPG, DH)
    gates = jax.nn.sigmoid(g.astype(f32)).reshape(bsz, seq, N_BRANCH, G, HPG).astype(h.dtype)

    def to_heads(t):
        return t.reshape(bsz, seq, G, DH)

    def compress(tok, j):
        chunks = to_heads(tok).reshape(bsz, seq // CMP_STRIDE, CMP_STRIDE, G, DH)
        blocks = jnp.concatenate([chunks[:, :-1], chunks[:, 1:]], axis=2) + cmp_pe[j][None, None, :, None, :]
        hid = jax.nn.gelu(jnp.einsum('bnlgd,lde->bnge', blocks, cmp_w1[j]) + cmp_b1[j])
        return jnp.einsum('bnge,ef->bngf', hid, cmp_w2[j])

    k_cmp, v_cmp = compress(kc_tok, 0), compress(vc_tok, 1)
    n_cmp = seq // CMP_STRIDE - 1
    cmp_end = jnp.arange(n_cmp) * CMP_STRIDE + CMP_BLOCK - 1
    n_sel = seq // SEL_BLOCK
    n_top = min(SEL_TOP_N, n_sel)

    def to_blocks(t):
        return to_heads(t).reshape(bsz, n_sel, SEL_BLOCK, G, DH).transpose(0, 3, 1, 2, 4)

    k_slc, v_slc = to_blocks(ks_tok), to_blocks(vs_tok)
    pad = ((0, 0), (WINDOW, 0), (0, 0), (0, 0))
    k_win, v_win = jnp.pad(to_heads(kw_tok), pad), jnp.pad(to_heads(vw_tok), pad)

    def bias_heads(b):
        return b.transpose(2, 0, 1).reshape(G, HPG, b.shape[0], b.shape[1])

    tbl_grp = rel_bias.T.reshape(G, HPG, T5_BUCKETS)
    b_ix = jnp.arange(bsz)[:, None, None, None]
    g_ix = jnp.arange(G)[None, :, None, None]
    h_ix = jnp.arange(HPG)[None, None, :, None, None]
    blk_ids = jnp.arange(n_sel)
    blk_start = blk_ids * SEL_BLOCK
    win_off = jnp.arange(WINDOW + QB) - WINDOW

    def block_fn(qb):
        t0 = qb * QB
        qpos = t0 + jnp.arange(QB)
        qblk = lax.dynamic_slice_in_dim(q, t0, QB, axis=1)
        gblk = lax.dynamic_slice_in_dim(gates, t0, QB, axis=1)
        dist_c = qpos[:, None] - cmp_end[None, :]
        s_c = jnp.einsum('bqghd,bngd->bghqn', qblk, k_cmp).astype(f32) + bias_heads(rel_bias[_t5_bucket(dist_c)])
        p_c = _masked_softmax(s_c, dist_c >= 0)
        o_c = jnp.einsum('bghqn,bngd->bqghd', p_c.astype(v_cmp.dtype), v_cmp)
        p_grp = jnp.pad(p_c.sum(axis=2), ((0, 0), (0, 0), (0, 0), (1, SEL_RATIO)))
        sel_score = p_grp[..., :SEL_RATIO * n_sel].reshape(bsz, G, QB, n_sel, SEL_RATIO).sum(-1) + p_grp[..., SEL_RATIO::SEL_RATIO]
        cur = (qpos // SEL_BLOCK)[:, None]
        visible = blk_start[None, :] <= qpos[:, None]
        forced = (blk_ids[None, :] == 0) | (blk_ids[None, :] == cur) | (blk_ids[None, :] == cur - 1)
        sel_score = jnp.where(forced, SEL_FORCE, jnp.where(visible, sel_score, -SEL_FORCE))
        _, sel_idx = lax.top_k(sel_score, n_top)
        ks = k_slc[b_ix, g_ix, sel_idx].reshape(bsz, G, QB, n_top * SEL_BLOCK, DH)
        vs = v_slc[b_ix, g_ix, sel_idx].reshape(bsz, G, QB, n_top * SEL_BLOCK, DH)
        kpos = (sel_idx[..., None] * SEL_BLOCK + jnp.arange(SEL_BLOCK)).reshape(bsz, G, QB, n_top * SEL_BLOCK)
        dist_s = qpos[None, None, :, None] - kpos
        bias_s = tbl_grp[g_ix[..., None], h_ix, _t5_bucket(dist_s)[:, :, None]]
        s_s = jnp.einsum('bqghd,bgqld->bghql', qblk, ks).astype(f32) + bias_s
        p_s = _masked_softmax(s_s, (dist_s >= 0)[:, :, None])
        o_s = jnp.einsum('bghql,bgqld->bqghd', p_s.astype(vs.dtype), vs)
        kw = lax.dynamic_slice_in_dim(k_win, t0, WINDOW + QB, axis=1)
        vw = lax.dynamic_slice_in_dim(v_win, t0, WINDOW + QB, axis=1)
        kpos_w = t0 + win_off
        dist_w = qpos[:, None] - kpos_w[None, :]
        mask_w = (dist_w >= 0) & (dist_w < WINDOW) & (kpos_w >= 0)[None, :]
        s_w = jnp.einsum('bqghd,bkgd->bghqk', qblk, kw).astype(f32) + bias_heads(rel_bias[_t5_bucket(dist_w)])
        p_w = _masked_softmax(s_w, mask_w)
        o_w = jnp.einsum('bghqk,bkgd->bqghd', p_w.astype(vw.dtype), vw)
        o = gblk[:, :, 0, :, :, None] * o_c + gblk[:, :, 1, :, :, None] * o_s + gblk[:, :, 2, :, :, None] * o_w
        return o.reshape(bsz, QB, NSA_Q_WIDTH)

    out = lax.map(block_fn, jnp.arange(seq // QB))
    out = out.transpose(1, 0, 2, 3).reshape(bsz, seq, NSA_Q_WIDTH)
    return (out @ w_out).astype(h.dtype)


def retention_mixer(h, w_in, gn_w, gn_b, w_out):
    bsz, seq, _ = h.shape
    f32 = jnp.float32
    q, k, v, g = jnp.split(h @ w_in, list(RET_SPLITS), axis=-1)
    pos = jnp.arange(seq)
    q = _rotary(q.reshape(bsz, seq, RET_HEADS, RET_QK_DIM), pos)
    k = _rotary(k.reshape(bsz, seq, RET_HEADS, RET_QK_DIM), pos) * RET_QK_DIM ** -0.5
    v = v.reshape(bsz, seq, RET_HEADS, RET_V_DIM)
    n_chunks = seq // RET_CHUNK

    def to_chunks(t):
        return t.astype(f32).reshape(bsz, n_chunks, RET_CHUNK, RET_HEADS, t.shape[-1]).transpose(1, 0, 3, 2, 4)

    log_g = jnp.log1p(-jnp.exp2(-5.0 - jnp.arange(RET_HEADS, dtype=f32)))
    idx = jnp.arange(RET_CHUNK, dtype=f32)
    diff = idx[:, None] - idx[None, :]
    decay = jnp.where(diff >= 0, jnp.exp(jnp.maximum(diff, 0.0) * log_g[:, None, None]), 0.0)
    xi = jnp.exp((idx + 1.0) * log_g[:, None])
    zeta = jnp.exp((RET_CHUNK - 1.0 - idx) * log_g[:, None])
    chunk_decay = jnp.exp(RET_CHUNK * log_g)

    def step(state, qkv):
        qc, kc, vc = qkv
        inner = jnp.einsum('bhqd,bhkd->bhqk', qc, kc) * decay
        out = jnp.einsum('bhqk,bhkv->bhqv', inner, vc) + jnp.einsum('bhqd,bhdv->bhqv', qc, state) * xi[:, :, None]
        state = state * chunk_decay[:, None, None] + jnp.einsum('bhkd,bhkv->bhdv', kc * zeta[:, :, None], vc)
        return state, out

    state0 = jnp.zeros((bsz, RET_HEADS, RET_QK_DIM, RET_V_DIM), f32)
    _, o = lax.scan(step, state0, (to_chunks(q), to_chunks(k), to_chunks(v)))
    o = o.transpose(1, 0, 3, 2, 4).reshape(bsz, seq, RET_HEADS, RET_V_DIM)
    mu = o.mean(-1, keepdims=True)
    var = jnp.mean(jnp.square(o - mu), -1, keepdims=True)
    o = ((o - mu) * lax.rsqrt(var + GN_EPS)).reshape(bsz, seq, RET_V_TOTAL) * gn_w + gn_b
    return ((jax.nn.silu(g) * o.astype(h.dtype)) @ w_out).astype(h.dtype)


def sq_relu_mlp(h, w1, w2):
    return (jnp.square(jax.nn.relu(h @ w1)) @ w2).astype(h.dtype)


def setup_inputs(seed: int = 0) -> dict:
    key = jax.random.key(seed)
    ks = jax.random.split(key, 26)
    f32 = jnp.float32

    def nrm(k, shape, scale):
        return jax.random.normal(k, shape, f32) * scale

    u = jax.random.uniform(ks[15], (N_LRU_LAYERS, LRU_WIDTH), f32, minval=0.9, maxval=0.999)
    a0 = u ** (1.0 / LRU_C)
    return {
        'x': nrm(ks[0], (BATCH, SEQ, D_MODEL), 1.0),
        'c': nrm(ks[1], (BATCH, D_MODEL), 1.0),
        'rel_bias': nrm(ks[2], (T5_BUCKETS, NSA_HEADS), 0.5),
        'ada_w': nrm(ks[3], (DEPTH, 2, D_MODEL, 3 * D_MODEL), 0.1 * D_MODEL ** -0.5),
        'ada_b': nrm(ks[4], (DEPTH, 2, 3 * D_MODEL), 0.02),
        'ln_w': 1.0 + nrm(ks[5], (DEPTH, 2, D_MODEL), 0.02),
        'ln_b': nrm(ks[6], (DEPTH, 2, D_MODEL), 0.02),
        'mlp_w1': nrm(ks[7], (DEPTH, D_MODEL, D_FF), D_MODEL ** -0.5),
        'mlp_w2': nrm(ks[8], (DEPTH, D_FF, D_MODEL), DN_BETA * D_FF ** -0.5),
        'lru_w_in': nrm(ks[9], (N_LRU_LAYERS, D_MODEL, 2 * LRU_WIDTH), D_MODEL ** -0.5),
        'lru_conv_w': nrm(ks[10], (N_LRU_LAYERS, CONV_WIDTH, LRU_WIDTH), CONV_WIDTH ** -0.5),
        'lru_conv_b': nrm(ks[11], (N_LRU_LAYERS, LRU_WIDTH), 0.02),
        'lru_gate_w': nrm(ks[12], (N_LRU_LAYERS, 2, LRU_BLOCKS, LRU_BLOCK_DIM, LRU_BLOCK_DIM), LRU_BLOCK_DIM ** -0.5),
        'lru_gate_b': nrm(ks[13], (N_LRU_LAYERS, 2, LRU_WIDTH), 0.02),
        'lru_lambda': jnp.log(a0) - jnp.log1p(-a0) + nrm(ks[14], (N_LRU_LAYERS, LRU_WIDTH), 0.01),
        'lru_w_out': nrm(ks[16], (N_LRU_LAYERS, LRU_WIDTH, D_MODEL), DN_BETA * LRU_WIDTH ** -0.5),
        'nsa_w_in': nrm(ks[17], (N_NSA_LAYERS, D_MODEL, NSA_IN_WIDTH), D_MODEL ** -0.5),
        'nsa_cmp_pe': nrm(ks[18], (N_NSA_LAYERS, 2, CMP_BLOCK, NSA_HEAD_DIM), 0.02),
        'nsa_cmp_w1': nrm(ks[19], (N_NSA_LAYERS, 2, CMP_BLOCK, NSA_HEAD_DIM, NSA_HEAD_DIM), (CMP_BLOCK * NSA_HEAD_DIM) ** -0.5),
        'nsa_cmp_b1': nrm(ks[20], (N_NSA_LAYERS, 2, NSA_HEAD_DIM), 0.02),
        'nsa_cmp_w2': nrm(ks[21], (N_NSA_LAYERS, 2, NSA_HEAD_DIM, NSA_HEAD_DIM), NSA_HEAD_DIM ** -0.5),
        'nsa_w_out': nrm(ks[22], (N_NSA_LAYERS, NSA_Q_WIDTH, D_MODEL), DN_BETA * NSA_Q_WIDTH ** -0.5),
        'ret_w_in': nrm(ks[23], (N_RET_LAYERS, D_MODEL, RET_IN_WIDTH), D_MODEL ** -0.5),
        'ret_gn_w': 1.0 + nrm(ks[24], (N_RET_LAYERS, 2, RET_V_TOTAL), 0.02)[:, 0],
        'ret_gn_b': nrm(ks[24], (N_RET_LAYERS, 2, RET_V_TOTAL), 0.02)[:, 1],
        'ret_w_out': nrm(ks[25], (N_RET_LAYERS, RET_V_TOTAL, D_MODEL), DN_BETA * RET_V_TOTAL ** -0.5),
    }


def reference(x, c, rel_bias, ada_w, ada_b, ln_w, ln_b, mlp_w1, mlp_w2,
              lru_w_in, lru_conv_w, lru_conv_b, lru_gate_w, lru_gate_b, lru_lambda, lru_w_out,
              nsa_w_in, nsa_cmp_pe, nsa_cmp_w1, nsa_cmp_b1, nsa_cmp_w2, nsa_w_out,
              ret_w_in, ret_gn_w, ret_gn_b, ret_w_out):
    cond = jax.nn.silu(c)
    for layer in range(DEPTH):
        mixer, inst = layer % N_MIXERS, layer // N_MIXERS
        mod = jnp.einsum('bd,jde->bje', cond, ada_w[layer]) + ada_b[layer]
        shift, scale, gate = jnp.split(mod, 3, axis=-1)
        h = x * (1.0 + scale[:, 0, None]) + shift[:, 0, None]
        if mixer == 0:
            y = rglru_mixer(h, lru_w_in[inst], lru_conv_w[inst], lru_conv_b[inst], lru_gate_w[inst],
                            lru_gate_b[inst], lru_lambda[inst], lru_w_out[inst])
        elif mixer == 1:
            y = nsa_mixer(h, rel_bias, nsa_w_in[inst], nsa_cmp_pe[inst], nsa_cmp_w1[inst],
                          nsa_cmp_b1[inst], nsa_cmp_w2[inst], nsa_w_out[inst])
        else:
            y = retention_mixer(h, ret_w_in[inst], ret_gn_w[inst], ret_gn_b[inst], ret_w_out[inst])
        x = _layer_norm(DN_ALPHA * x + (1.0 + gate[:, 0, None]) * y, ln_w[layer, 0], ln_b[layer, 0])
        h = x * (1.0 + scale[:, 1, None]) + shift[:, 1, None]
        y = sq_relu_mlp(h, mlp_w1[layer], mlp_w2[layer])
        x = _layer_norm(DN_ALPHA * x + (1.0 + gate[:, 1, None]) * y, ln_w[layer, 1], ln_b[layer, 1])
    return x
```

```python
import math
from contextlib import ExitStack

import numpy as np
import ml_dtypes
import concourse.bass as bass
import concourse.mybir as mybir
from concourse.bass_utils import run_bass_kernel_spmd

F32 = mybir.dt.float32
BF16 = mybir.dt.bfloat16
ALU = mybir.AluOpType
AF = mybir.ActivationFunctionType
AX = mybir.AxisListType
NPBF = ml_dtypes.bfloat16

D = 2048
S = 8192
B = 2
NTOK = B * S
DEPTH = 4
DFF = 8192
DN_ALPHA = (2 * DEPTH) ** 0.25
LN_EPS = 1e-5
NCORE = 8
TPC = NTOK // NCORE


class Dep:
    __slots__ = ("w", "r")

    def __init__(self):
        self.w = None
        self.r = {}


class Buf:
    def __init__(self, t, nsub=1):
        self.t = t
        self.d = [Dep() for _ in range(nsub)]

    def __getitem__(self, k):
        return self.t[k]


class Prog:
    ENGS = ("tensor", "vector", "scalar", "gpsimd", "sync")
    NRING = 8

    def __init__(self):
        self.nc = bass.Bass("TRN2", target_bir_lowering=False)
        self.es0 = ExitStack()
        self.stacks = [ExitStack()]
        self.ops = {e: [] for e in self.ENGS}
        self.cnt = {e: 0 for e in self.ENGS}
        self.waited = {e: {} for e in self.ENGS}
        self.sem = {}
        for e in ("tensor", "vector", "scalar", "gpsimd"):
            self.sem[e] = self.es0.enter_context(self.nc.semaphore("s_" + e))
        self.ring = {}
        self.ringuse = {}
        for q in ("sync", "gpsimd", "scalar"):
            self.ring[q] = [self.es0.enter_context(self.nc.semaphore("d_%s%d" % (q, i)))
                            for i in range(self.NRING)]
            self.ringuse[q] = 0
        self.nbuf = 0

    def dram(self, name, shape, dt, kind="ExternalInput", nsub=1):
        return Buf(self.nc.dram_tensor(name, list(shape), dt, kind=kind), nsub)

    def sb(self, shape, dt, nsub=1):
        self.nbuf += 1
        t = self.stacks[-1].enter_context(self.nc.sbuf_tensor("sb%d" % self.nbuf, list(shape), dt))
        return Buf(t, nsub)

    def ps(self, shape, dt=F32, nsub=1):
        self.nbuf += 1
        t = self.stacks[-1].enter_context(self.nc.psum_tensor("ps%d" % self.nbuf, list(shape), dt))
        return Buf(t, nsub)

    def push(self):
        self.stacks.append(ExitStack())

    def pop(self):
        self.barrier()
        self.stacks.pop().close()

    def _deps(self, reads, writes):
        deps = {}

        def add(tok):
            if tok is None:
                return
            k, v = tok
            if deps.get(k, 0) < v:
                deps[k] = v
        for d in reads:
            add(d.w)
        for d in writes:
            add(d.w)
            for k, v in d.r.items():
                add((k, v))
        return deps

    def _commit(self, tok, reads, writes):
        k, v = tok
        for d in reads:
            if d.r.get(k, 0) < v:
                d.r[k] = v
        for d in writes:
            d.w = tok
            d.r = {}

    @staticmethod
    def _flat(lst):
        out = []
        for x in lst:
            if isinstance(x, Buf):
                out.extend(x.d)
            elif isinstance(x, Dep):
                out.append(x)
            elif isinstance(x, (list, tuple)):
                out.extend(Prog._flat(x))
            elif x is None:
                pass
            else:
                raise TypeError(x)
        return out

    def _waits(self, eng, deps, skip_self_pe=True):
        waits = []
        for k, v in deps.items():
            if skip_self_pe and eng == "tensor" and k is self.sem["tensor"]:
                continue
            if self.waited[eng].get(k, 0) >= v:
                continue
            self.waited[eng][k] = v
            waits.append((k, v))
        return waits

    def op(self, eng, fn, reads=(), writes=()):
        reads = self._flat(reads)
        writes = self._flat(writes)
        waits = self._waits(eng, self._deps(reads, writes))
        self.cnt[eng] += 1
        tok = (self.sem[eng], self.cnt[eng])
        self.ops[eng].append((waits, fn, (self.sem[eng], 1)))
        self._commit(tok, reads, writes)

    def v(self, eng, name, *args, reads=(), writes=(), **kw):
        self.op(eng, lambda e: getattr(e, name)(*args, **kw), reads, writes)

    def dma(self, q, out_ap, in_ap, reads=(), writes=(), **kw):
        reads = self._flat(reads)
        writes = self._flat(writes)
        deps = self._deps(reads, writes)
        i = self.ringuse[q]
        self.ringuse[q] += 1
        s = self.ring[q][i % self.NRING]
        prev = 16 * (i // self.NRING)
        if prev > 0 and deps.get(s, 0) < prev:
            deps[s] = prev
        waits = self._waits(q, deps)
        tok = (s, prev + 16)
        self.ops[q].append((waits, lambda e: e.dma_start(out=out_ap, in_=in_ap, **kw), (s, 16)))
        self._commit(tok, reads, writes)

    def _all_tokens(self):
        toks = {}
        for e in ("tensor", "vector", "scalar", "gpsimd"):
            if self.cnt[e] > 0:
                toks[self.sem[e]] = self.cnt[e]
        for q, ring in self.ring.items():
            n = self.ringuse[q]
            for i, s in enumerate(ring):
                c = (n - i + self.NRING - 1) // self.NRING
                if c > 0:
                    toks[s] = 16 * c
        return toks

    def barrier(self, engs=None):
        toks = self._all_tokens()
        for e in (engs or self.ENGS):
            waits = self._waits(e, dict(toks), skip_self_pe=False)
            if waits:
                self.ops[e].append((waits, None, None))

    def finish(self):
        self.barrier()
        nc = self.nc
        ops = self.ops
        with nc.Block() as block:
            def replay(name):
                def f(e):
                    for waits, fn, inc in ops[name]:
                        for (s, v) in waits:
                            e.wait_ge(s, v)
                        if fn is not None:
                            fn(e).then_inc(inc[0], inc[1])
                return f
            block.tensor(replay("tensor"))
            block.vector(replay("vector"))
            block.scalar(replay("scalar"))
            block.gpsimd(replay("gpsimd"))
            block.sync(replay("sync"))
        while self.stacks:
            self.stacks.pop().close()
        self.es0.close()
        return nc

    def mm(self, out, lhsT, rhs, start, stop, reads=(), writes=()):
        self.op("tensor", lambda e: e.matmul(out, lhsT, rhs, start=start, stop=stop), reads, writes)

    def act(self, out, in_, func, reads=(), writes=(), **kw):
        self.op("scalar", lambda e: e.activation(out, in_, func, **kw), reads, writes)


def _launch(P, in_maps):
    res = run_bass_kernel_spmd(P.nc, in_maps, core_ids=list(range(len(in_maps))))
    return res.results


def vec16(v):
    return np.ascontiguousarray(np.asarray(v, np.float32).reshape(16, 128).T)


def build_p0():
    P = Prog()
    cT = P.dram("cT", [128, 16, 2], F32)
    W = P.dram("W", [D, 3 * D], F32)
    bias = P.dram("bias", [2, 3 * D], F32)
    out = P.dram("mod", [2, 3 * D], F32, kind="ExternalOutput")
    c = P.sb([128, 16, 2], F32)
    cs = P.sb([128, 16, 2], F32)
    bsb = P.sb([2, 3 * D], F32)
    osb = P.sb([2, 3 * D], F32)
    P.dma("sync", c[:], cT[:, :, :], writes=[c])
    P.dma("sync", bsb[:], bias[:, :], writes=[bsb])
    P.act(cs[:], c[:], AF.Silu, reads=[c], writes=[cs])
    wb = [P.sb([128, 16, 512], F32, nsub=2) for _ in range(2)]
    pss = [P.ps([2, 512]) for _ in range(2)]
    for n in range(12):
        w = wb[n % 2]
        for hh in range(2):
            P.dma("sync" if hh == 0 else "gpsimd", w[:, hh * 8:(hh + 1) * 8, :],
                  W[hh * 1024:(hh + 1) * 1024, n * 512:(n + 1) * 512].rearrange("(k p) n -> p k n", p=128),
                  writes=[w.d[hh]])
        ps = pss[n % 2]
        for k in range(16):
            P.mm(ps[:, :], cs[:, k, :], w[:, k, :], k == 0, k == 15, reads=[cs, w], writes=[ps])
        P.v("vector", "tensor_tensor", osb[:, n * 512:(n + 1) * 512], ps[:, :], bsb[:, n * 512:(n + 1) * 512],
            ALU.add, reads=[ps, bsb], writes=[osb])
    P.dma("sync", out[:, :], osb[:], reads=[osb], writes=[out])
    return P.finish()


def build_p1(N, out_dt):
    P = Prog()
    xT = P.dram("xT", [D, TPC], F32)
    vec = P.dram("vec", [128, 2, 16], F32)
    W = P.dram("W", [D, N], F32)
    outT = P.dram("outT", [N, TPC], out_dt, kind="ExternalOutput")
    v = P.sb([128, 2, 16], F32)
    s1 = P.sb([128, 16], F32)
    P.dma("sync", v[:], vec[:, :, :], writes=[v])
    P.v("vector", "tensor_scalar_add", s1[:], v[:, 0, :], 1.0, reads=[v], writes=[s1])
    hT = P.sb([128, 16, TPC], BF16, nsub=4)
    P.push()
    xs = [P.sb([128, 16, 512], F32, nsub=2) for _ in range(2)]
    for tt in range(4):
        xb = xs[tt % 2]
        for hh in range(2):
            P.dma("sync", xb[:, hh * 8:(hh + 1) * 8, :],
                  xT[hh * 1024:(hh + 1) * 1024, tt * 512:(tt + 1) * 512].rearrange("(k p) t -> p k t", p=128),
                  writes=[xb.d[hh]])
        for k in range(16):
            P.v("vector", "tensor_scalar", hT[:, k, tt * 512:(tt + 1) * 512],
                xb[:, k, :], s1[:, k:k + 1], v[:, 1, k:k + 1], ALU.mult, ALU.add,
                reads=[xb.d[k // 8], s1, v], writes=[hT.d[tt]])
    P.pop()
    NW = 512
    wb = [P.sb([128, 16, NW], BF16, nsub=2) for _ in range(3)]
    pss = [P.ps([128, 512]) for _ in range(4)]
    osb = [P.sb([128, 512], out_dt) for _ in range(4)]
    it = 0
    for n0 in range(0, N, NW):
        nw = min(NW, N - n0)
        w = wb[(n0 // NW) % 3]
        for hh in range(2):
            P.dma("gpsimd", w[:, hh * 8:(hh + 1) * 8, :nw],
                  W[hh * 1024:(hh + 1) * 1024, n0:n0 + nw].rearrange("(k p) n -> p k n", p=128), writes=[w.d[hh]])
        for c0 in range(0, nw, 128):
            m = min(128, nw - c0)
            for tt in range(4):
                ps = pss[it % 4]
                o = osb[it % 4]
                for k in range(16):
                    P.mm(ps[:m, :], w[:, k, c0:c0 + m], hT[:, k, tt * 512:(tt + 1) * 512], k == 0, k == 15,
                         reads=[w, hT.d[tt]], writes=[ps])
                if it % 2 == 0:
                    P.v("vector", "tensor_copy", o[:m, :], ps[:m, :], reads=[ps], writes=[o])
                else:
                    P.v("scalar", "copy", o[:m, :], ps[:m, :], reads=[ps], writes=[o])
                P.dma("sync", outT[n0 + c0:n0 + c0 + m, tt * 512:(tt + 1) * 512], o[:m, :], reads=[o])
                it += 1
    return P.finish()


def _ln_tile(P, z, ones, vv, iw, ib, ps_m, ps_e, tmp):
    sq, mean, msq, var, rstd = tmp
    for k in range(16):
        s = sq[k % 2]
        P.act(s[:], z[:, k, :], AF.Square, reads=[z.d[k]], writes=[s])
        P.mm(ps_m[:, :], ones[:], z[:, k, :], k == 0, k == 15, reads=[ones, z.d[k]], writes=[ps_m])
        P.mm(ps_e[:, :], ones[:], s[:], k == 0, k == 15, reads=[ones, s], writes=[ps_e])
    P.v("scalar", "copy", mean[:], ps_m[:, :], reads=[ps_m], writes=[mean])
    P.v("vector", "tensor_tensor", msq[:], mean[:], mean[:], ALU.mult, reads=[mean], writes=[msq])
    P.v("vector", "tensor_tensor", var[:], ps_e[:, :], msq[:], ALU.subtract, reads=[ps_e, msq], writes=[var])
    P.v("vector", "tensor_scalar", var[:], var[:], 0.0, LN_EPS, ALU.max, ALU.add, reads=[var], writes=[var])
    P.act(msq[:], var[:], AF.Sqrt, reads=[var], writes=[msq])
    P.v("vector", "reciprocal", rstd[:], msq[:], reads=[msq], writes=[rstd])
    for k in range(16):
        P.v("vector", "tensor_tensor", z[:, k, :], z[:, k, :], mean[:], ALU.subtract,
            reads=[z.d[k], mean], writes=[z.d[k]])
        P.v("vector", "tensor_tensor", z[:, k, :], z[:, k, :], rstd[:], ALU.mult,
            reads=[z.d[k], rstd], writes=[z.d[k]])
        P.act(z[:, k, :], z[:, k, :], AF.Identity, scale=vv[:, iw, k:k + 1], bias=vv[:, ib, k:k + 1],
              reads=[z.d[k], vv], writes=[z.d[k]])


def build_p2(KIN, in_dt):
    KC = KIN // 128
    P = Prog()
    mixT = P.dram("mixT", [KIN, TPC], in_dt)
    Wo = P.dram("Wo", [KIN, D], F32)
    xT = P.dram("xT", [D, TPC], F32)
    vec = P.dram("vec", [128, 8, 16], F32)
    W1 = P.dram("W1", [D, DFF], F32)
    W2 = P.dram("W2", [16, 128, 64, 128], F32)
    x1T = P.dram("x1T", [D, TPC], F32, kind="Internal", nsub=4)
    x2T = P.dram("x2T", [D, TPC], F32, kind="ExternalOutput")
    vv = P.sb([128, 8, 16], F32)
    g1 = P.sb([128, 16], F32)
    s2 = P.sb([128, 16], F32)
    g2 = P.sb([128, 16], F32)
    ones = P.sb([128, 128], F32)
    P.dma("sync", vv[:], vec[:, :, :], writes=[vv])
    P.v("vector", "tensor_scalar_add", g1[:], vv[:, 0, :], 1.0, reads=[vv], writes=[g1])
    P.v("vector", "tensor_scalar_add", s2[:], vv[:, 1, :], 1.0, reads=[vv], writes=[s2])
    P.v("vector", "tensor_scalar_add", g2[:], vv[:, 3, :], 1.0, reads=[vv], writes=[g2])
    P.v("vector", "memset", ones[:], 1.0 / D, writes=[ones])
    sq = [P.sb([128, 512], F32) for _ in range(2)]
    tmp = (sq, P.sb([128, 512], F32), P.sb([128, 512], F32), P.sb([128, 512], F32), P.sb([128, 512], F32))
    yt = [P.sb([128, 512], F32) for _ in range(2)]
    ps_m = P.ps([128, 512])
    ps_e = P.ps([128, 512])
    pss = [P.ps([128, 512]) for _ in range(4)]

    P.push()
    NP = KC // 8
    mixb = [P.sb([128, KC, 512], BF16, nsub=NP) for _ in range(2)]
    xb = [P.sb([128, 16, 512], F32, nsub=16) for _ in range(2)]
    wob = [P.sb([128, KC, 256], BF16, nsub=NP) for _ in range(2)]
    it = 0
    for tt in range(4):
        mx = mixb[tt % 2]
        x = xb[tt % 2]
        tsl = slice(tt * 512, (tt + 1) * 512)
        for hh in range(NP):
            P.dma("gpsimd" if in_dt != BF16 else "sync", mx[:, hh * 8:hh * 8 + 8, :],
                  mixT[hh * 1024:(hh + 1) * 1024, tsl].rearrange("(k p) t -> p k t", p=128), writes=[mx.d[hh]])
        for hh in range(2):
            P.dma("sync", x[:, hh * 8:(hh + 1) * 8, :],
                  xT[hh * 1024:(hh + 1) * 1024, tsl].rearrange("(k p) t -> p k t", p=128),
                  writes=x.d[hh * 8:(hh + 1) * 8])
        for dc in range(16):
            w = wob[(dc // 2) % 2]
            if dc % 2 == 0:
                for hh in range(NP):
                    P.dma("gpsimd", w[:, hh * 8:hh * 8 + 8, :],
                          Wo[hh * 1024:(hh + 1) * 1024, dc * 128:(dc + 2) * 128].rearrange("(k p) n -> p k n", p=128),
                          writes=[w.d[hh]])
            ps = pss[it % 4]
            y = yt[it % 2]
            it += 1
            co = (dc % 2) * 128
            for k in range(KC):
                P.mm(ps[:, :], w[:, k, co:co + 128], mx[:, k, :], k == 0, k == KC - 1, reads=[w, mx], writes=[ps])
            P.act(y[:], ps[:, :], AF.Identity, scale=g1[:, dc:dc + 1], reads=[ps, g1], writes=[y])
            P.v("vector", "scalar_tensor_tensor", x[:, dc, :], x[:, dc, :], DN_ALPHA, y[:], ALU.mult, ALU.add,
                reads=[x.d[dc], y], writes=[x.d[dc]])
        _ln_tile(P, x, ones, vv, 4, 5, ps_m, ps_e, tmp)
        for hh in range(2):
            P.dma("sync", x1T[hh * 1024:(hh + 1) * 1024, tsl].rearrange("(k p) t -> p k t", p=128),
                  x[:, hh * 8:(hh + 1) * 8, :], reads=x.d[hh * 8:(hh + 1) * 8], writes=[x1T.d[tt]])
    P.pop()

    P.push()
    x = P.sb([128, 16, 512], F32, nsub=16)
    h2 = P.sb([128, 16, 512], BF16, nsub=16)
    h1 = P.sb([128, 64, 512], BF16, nsub=64)
    w1b = [P.sb([128, 16, 512], BF16, nsub=2) for _ in range(2)]
    w2b = [P.sb([128, 64, 128], BF16, nsub=2) for _ in range(2)]
    rl = [P.sb([128, 512], F32) for _ in range(2)]
    it = 0
    for tt in range(4):
        tsl = slice(tt * 512, (tt + 1) * 512)
        for hh in range(2):
            P.dma("sync", x[:, hh * 8:(hh + 1) * 8, :],
                  x1T[hh * 1024:(hh + 1) * 1024, tsl].rearrange("(k p) t -> p k t", p=128),
                  reads=[x1T.d[tt]], writes=x.d[hh * 8:(hh + 1) * 8])
        for k in range(16):
            P.v("vector", "tensor_scalar", h2[:, k, :], x[:, k, :],
                s2[:, k:k + 1], vv[:, 2, k:k + 1], ALU.mult, ALU.add, reads=[x.d[k], s2, vv], writes=[h2.d[k]])
        for f0 in range(0, 64, 4):
            w = w1b[(f0 // 4) % 2]
            for hh in range(2):
                P.dma("gpsimd", w[:, hh * 8:(hh + 1) * 8, :],
                      W1[hh * 1024:(hh + 1) * 1024, f0 * 128:(f0 + 4) * 128].rearrange("(k p) n -> p k n", p=128),
                      writes=[w.d[hh]])
            for fi in range(4):
                f = f0 + fi
                ps = pss[it % 4]
                r = rl[it % 2]
                it += 1
                for k in range(16):
                    P.mm(ps[:, :], w[:, k, fi * 128:(fi + 1) * 128], h2[:, k, :], k == 0, k == 15,
                         reads=[w, h2.d[k]], writes=[ps])
                P.act(r[:], ps[:, :], AF.Relu, reads=[ps], writes=[r])
                P.v("vector", "tensor_tensor", h1[:, f, :], r[:], r[:], ALU.mult, reads=[r], writes=[h1.d[f]])
        for dc in range(16):
            w = w2b[dc % 2]
            for hh in range(2):
                P.dma("gpsimd", w[:, hh * 32:(hh + 1) * 32, :], W2[dc, :, hh * 32:(hh + 1) * 32, :], writes=[w.d[hh]])
            ps = pss[it % 4]
            y = yt[it % 2]
            it += 1
            for f in range(64):
                P.mm(ps[:, :], w[:, f, :], h1[:, f, :], f == 0, f == 63, reads=[w, h1.d[f]], writes=[ps])
            P.act(y[:], ps[:, :], AF.Identity, scale=g2[:, dc:dc + 1], reads=[ps, g2], writes=[y])
            P.v("vector", "scalar_tensor_tensor", x[:, dc, :], x[:, dc, :], DN_ALPHA, y[:], ALU.mult, ALU.add,
                reads=[x.d[dc], y], writes=[x.d[dc]])
        _ln_tile(P, x, ones, vv, 6, 7, ps_m, ps_e, tmp)
        for hh in range(2):
            P.dma("sync", x2T[hh * 1024:(hh + 1) * 1024, tsl].rearrange("(k p) t -> p k t", p=128),
                  x[:, hh * 8:(hh + 1) * 8, :], reads=x.d[hh * 8:(hh + 1) * 8])
    P.pop()
    return P.finish()


def w2_layout(W2):
    return np.ascontiguousarray(W2.reshape(64, 128, 16, 128).transpose(2, 1, 0, 3))


def build_lru():
    TT = 512
    P = Prog()
    xbr = P.dram("xbr", [256, NTOK], F32)
    ybr = P.dram("ybr", [256, NTOK], F32)
    cvec = P.dram("cvec", [128, 2, 8], F32)
    gw = P.dram("gw", [2, 256, 256], F32)
    mixT = P.dram("mixT", [256, NTOK], BF16, kind="ExternalOutput")
    cv = P.sb([128, 2, 8], F32)
    gwb = P.sb([128, 2, 2, 256], BF16)
    P.dma("sync", cv[:], cvec[:, :, :], writes=[cv])
    for g in range(2):
        P.dma("gpsimd", gwb[:, g, :, :], gw[g].rearrange("(c p) e -> p c e", p=128), writes=[gwb])
    t1 = P.sb([128, 2], F32)
    nsp8 = P.sb([128, 2], F32)
    P.act(t1[:], cv[:, :, 7], AF.Exp, scale=-1.0, reads=[cv], writes=[t1])
    P.act(t1[:], t1[:], AF.Ln, bias=1.0, reads=[t1], writes=[t1])
    P.v("vector", "tensor_scalar_mul", nsp8[:], t1[:], -8.0, reads=[t1], writes=[nsp8])

    def bufset():
        d = {}
        d["xin"] = [P.sb([128, TT + 3], F32) for _ in range(2)]
        d["y"] = [P.sb([128, TT], F32) for _ in range(2)]
        d["xc"] = [P.sb([128, TT], F32) for _ in range(2)]
        d["xcb"] = [P.sb([128, TT], BF16) for _ in range(2)]
        d["r"] = [P.sb([128, TT], F32) for _ in range(2)]
        d["i"] = [P.sb([128, TT], F32) for _ in range(2)]
        d["a"] = [P.sb([128, TT], F32) for _ in range(2)]
        d["u"] = [P.sb([128, TT], F32) for _ in range(2)]
        d["hs"] = [P.sb([128, TT], F32) for _ in range(2)]
        d["o"] = [P.sb([128, TT], BF16) for _ in range(2)]
        return d
    sets = [bufset(), bufset()]
    pss = [P.ps([128, TT]) for _ in range(8)]
    it = 0
    for b in range(B):
        for tt in range(S // TT):
            bs = sets[it % 2]
            prev = sets[(it + 1) % 2]
            tok0 = b * S + tt * TT
            for pc in range(2):
                xin = bs["xin"][pc]
                rows = slice(pc * 128, (pc + 1) * 128)
                if tt == 0:
                    P.v("vector", "memset", xin[:, 0:3], 0.0, writes=[xin])
                    P.dma("sync", xin[:, 3:TT + 3], xbr[rows, tok0:tok0 + TT], writes=[xin])
                else:
                    P.dma("sync", xin[:, :], xbr[rows, tok0 - 3:tok0 + TT], writes=[xin])
                P.dma("sync", bs["y"][pc][:], ybr[rows, tok0:tok0 + TT], writes=[bs["y"][pc]])
                xc = bs["xc"][pc]
                P.v("vector", "tensor_scalar", xc[:], xin[:, 0:TT], cv[:, pc, 0:1], cv[:, pc, 4:5], ALU.mult, ALU.add,
                    reads=[xin, cv], writes=[xc])
                for k in range(1, 4):
                    P.v("vector", "scalar_tensor_tensor", xc[:], xin[:, k:k + TT], cv[:, pc, k:k + 1], xc[:],
                        ALU.mult, ALU.add, reads=[xin, cv, xc], writes=[xc])
                P.v("scalar", "copy", bs["xcb"][pc][:], xc[:], reads=[xc], writes=[bs["xcb"][pc]])
            for ec in range(2):
                for g in range(2):
                    ps = pss[(it % 2) * 4 + ec * 2 + g]
                    for dc in range(2):
                        P.mm(ps[:, :], gwb[:, g, dc, ec * 128:(ec + 1) * 128], bs["xcb"][dc][:], dc == 0, dc == 1,
                             reads=[gwb, bs["xcb"][dc]], writes=[ps])
                    dst = bs["r"][ec] if g == 0 else bs["i"][ec]
                    P.act(dst[:], ps[:, :], AF.Sigmoid, bias=cv[:, ec, 5 + g:6 + g], reads=[ps, cv], writes=[dst])
            for ec in range(2):
                P.act(bs["a"][ec][:], bs["r"][ec][:], AF.Exp, scale=nsp8[:, ec:ec + 1],
                      reads=[bs["r"][ec], nsp8], writes=[bs["a"][ec]])
            for ec in range(2):
                a, r, i_, u, xc = bs["a"][ec], bs["r"][ec], bs["i"][ec], bs["u"][ec], bs["xc"][ec]
                P.v("vector", "tensor_tensor", r[:], a[:], a[:], ALU.mult, reads=[a], writes=[r])
                P.v("vector", "tensor_scalar", r[:], r[:], -1.0, 1.0, ALU.mult, ALU.add, reads=[r], writes=[r])
                P.v("vector", "tensor_tensor", i_[:], i_[:], xc[:], ALU.mult, reads=[i_, xc], writes=[i_])
            for ec in range(2):
                P.act(bs["r"][ec][:], bs["r"][ec][:], AF.Sqrt, reads=[bs["r"][ec]], writes=[bs["r"][ec]])
            for ec in range(2):
                a, r, i_, u, hs = bs["a"][ec], bs["r"][ec], bs["i"][ec], bs["u"][ec], bs["hs"][ec]
                P.v("vector", "tensor_tensor", u[:], r[:], i_[:], ALU.mult, reads=[r, i_], writes=[u])
                if tt == 0:
                    P.v("vector", "tensor_tensor_scan", hs[:], a[:], u[:], 0.0, ALU.mult, ALU.add,
                        reads=[a, u], writes=[hs])
                else:
                    ph = prev["hs"][ec]
                    P.v("vector", "tensor_tensor_scan", hs[:], a[:], u[:], ph[:, TT - 1:TT], ALU.mult, ALU.add,
                        reads=[a, u, ph], writes=[hs])
            for ec in range(2):
                y = bs["y"][ec]
                P.act(y[:], y[:], AF.Gelu_apprx_tanh, reads=[y], writes=[y])
            for ec in range(2):
                y, hs, o = bs["y"][ec], bs["hs"][ec], bs["o"][ec]
                P.v("vector", "tensor_tensor", o[:], y[:], hs[:], ALU.mult, reads=[y, hs], writes=[o])
                P.dma("sync", mixT[ec * 128:(ec + 1) * 128, tok0:tok0 + TT], o[:], reads=[o])
            it += 1
    return P.finish()


RET_C = 128
GN_EPS = 1e-5


def build_ret():
    C = RET_C
    P = Prog()
    qT = P.dram("qT", [256, NTOK], BF16)
    kT = P.dram("kT", [256, NTOK], BF16)
    vv_ = P.dram("v", [NTOK, 512], BF16)
    gg = P.dram("g", [NTOK, 512], BF16)
    cs_d = P.dram("cs", [128, 2, S], F32)
    dec_d = P.dram("decT", [128, 128], F32)
    xi_d = P.dram("xirow", [128, 128], F32)
    sc_d = P.dram("scal", [128, 2], F32)
    gn_d = P.dram("gn", [128, 2, 512], F32)
    id_d = P.dram("ident", [128, 128], BF16)
    out = P.dram("o", [NTOK, 512], BF16, kind="ExternalOutput")

    dec = P.sb([128, 128], F32)
    xi = P.sb([128, 128], F32)
    scal = P.sb([128, 2], F32)
    gn = P.sb([128, 2, 512], F32)
    ident = P.sb([128, 128], BF16)
    for (a, b_) in ((dec, dec_d), (xi, xi_d), (scal, sc_d), (ident, id_d)):
        P.dma("sync", a[:], b_[:, :], writes=[a])
    P.dma("sync", gn[:], gn_d[:, :, :], writes=[gn])
    state = P.sb([128, 2, 512], F32, nsub=2)
    state_bf = P.sb([128, 2, 512], BF16, nsub=2)

    def bufset():
        d = {}
        d["q"] = P.sb([128, 2, C], BF16)
        d["k"] = P.sb([128, 2, C], BF16)
        d["v"] = P.sb([128, 512], BF16)
        d["g"] = P.sb([128, 512], BF16)
        d["sg"] = P.sb([128, 512], F32)
        d["cs"] = P.sb([128, 2, C], F32)
        d["A"] = P.sb([128, 2, C], F32)
        d["Bm"] = P.sb([128, 2, C], F32)
        d["qr"] = P.sb([128, 2, C], BF16)
        d["kr"] = P.sb([128, 2, C], BF16)
        d["qxi"] = P.sb([128, 2, C], BF16)
        d["inT"] = P.sb([128, C], BF16)
        d["kz"] = P.sb([128, 256], BF16)
        d["o"] = P.sb([128, 512], F32)
        d["sq"] = P.sb([128, 512], F32)
        d["st"] = P.sb([128, 8], F32)
        d["ob"] = P.sb([128, 512], BF16)
        return d
    sets = [bufset(), bufset()]
    ps_in = [P.ps([128, C]) for _ in range(2)]
    ps_o = [P.ps([128, 512]) for _ in range(2)]
    ps_t = P.ps([128, 256], BF16)
    ps_s = [P.ps([128, 512]) for _ in range(2)]
    it = 0
    for b in range(B):
        for c in range(S // C):
            bs = sets[it % 2]
            tok0 = b * S + c * C
            s0 = c * C
            q, k, v, g, cs = bs["q"], bs["k"], bs["v"], bs["g"], bs["cs"]
            P.dma("sync", q[:], qT[:, tok0:tok0 + C].rearrange("(h p) t -> p h t", p=128), writes=[q])
            P.dma("sync", k[:], kT[:, tok0:tok0 + C].rearrange("(h p) t -> p h t", p=128), writes=[k])
            P.dma("sync", v[:], vv_[tok0:tok0 + C, :], writes=[v])
            P.dma("sync", g[:], gg[tok0:tok0 + C, :], writes=[g])
            P.dma("sync", cs[:], cs_d[:, :, s0:s0 + C], writes=[cs])
            A, Bm = bs["A"], bs["Bm"]
            cosb = cs[:, 0:1, :].to_broadcast([128, 2, C])
            sinb = cs[:, 1:2, :].to_broadcast([128, 2, C])
            for (src, dst) in ((q, bs["qr"]), (k, bs["kr"])):
                P.v("vector", "tensor_tensor", A[:], src[:], cosb, ALU.mult, reads=[src, cs], writes=[A])
                P.v("vector", "tensor_tensor", Bm[:], src[:], sinb, ALU.mult, reads=[src, cs], writes=[Bm])
                P.v("vector", "tensor_tensor", dst[:, 0, :], A[:, 0, :], Bm[:, 1, :], ALU.subtract,
                    reads=[A, Bm], writes=[dst])
                P.v("vector", "tensor_tensor", dst[:, 1, :], Bm[:, 0, :], A[:, 1, :], ALU.add,
                    reads=[A, Bm], writes=[dst])
            qr, kr, qxi = bs["qr"], bs["kr"], bs["qxi"]
            P.v("vector", "tensor_tensor", qxi[:], qr[:], xi[:].unsqueeze(1).to_broadcast([128, 2, C]), ALU.mult,
                reads=[qr, xi], writes=[qxi])
            pin = ps_in[it % 2]
            for hf in range(2):
                P.mm(pin[:, :], kr[:, hf, :], qr[:, hf, :], hf == 0, hf == 1, reads=[kr, qr], writes=[pin])
            inT = bs["inT"]
            P.v("vector", "tensor_tensor", inT[:], pin[:, :], dec[:], ALU.mult, reads=[pin, dec], writes=[inT])
            po = ps_o[it % 2]
            P.mm(po[:, :], inT[:], v[:], True, c == 0, reads=[inT, v], writes=[po])
            if c > 0:
                for hf in range(2):
                    P.mm(po[:, :], qxi[:, hf, :], state_bf[:, hf, :], False, hf == 1,
                         reads=[qxi, state_bf.d[hf]], writes=[po])
            kz = bs["kz"]
            for hf in range(2):
                P.op("tensor", lambda e, o_=ps_t[:, hf * 128:(hf + 1) * 128], i_=kr[:, hf, :]: e.transpose(o_, i_, ident[:]),
                     reads=[kr, ident], writes=[ps_t])
            P.act(kz[:], ps_t[:, :], AF.Copy, scale=scal[:, 0:1], reads=[ps_t, scal], writes=[kz])
            if c < S // C - 1:
                for hf in range(2):
                    pst = ps_s[hf]
                    P.mm(pst[:, :], kz[:, hf * 128:(hf + 1) * 128], v[:], True, True, reads=[kz, v], writes=[pst])
                    if c == 0:
                        P.v("vector", "tensor_copy", state[:, hf, :], pst[:, :], reads=[pst], writes=[state.d[hf]])
                    else:
                        P.v("vector", "scalar_tensor_tensor", state[:, hf, :], state[:, hf, :], scal[:, 1:2], pst[:, :],
                            ALU.mult, ALU.add, reads=[state.d[hf], scal, pst], writes=[state.d[hf]])
                    P.v("scalar", "copy", state_bf[:, hf, :], state[:, hf, :], reads=[state.d[hf]], writes=[state_bf.d[hf]])
            o, sq, st, ob = bs["o"], bs["sq"], bs["st"], bs["ob"]
            P.v("scalar", "copy", o[:], po[:, :], reads=[po], writes=[o])
            P.act(sq[:], o[:], AF.Square, reads=[o], writes=[sq])
            P.v("vector", "reduce_sum", st[:, 0:1], o[:], AX.X, reads=[o], writes=[st])
            P.v("vector", "reduce_sum", st[:, 1:2], sq[:], AX.X, reads=[sq], writes=[st])
            P.v("vector", "tensor_scalar_mul", st[:, 2:3], st[:, 0:1], 1.0 / 512, reads=[st], writes=[st])
            P.v("vector", "tensor_tensor", st[:, 3:4], st[:, 2:3], st[:, 2:3], ALU.mult, reads=[st], writes=[st])
            P.v("vector", "scalar_tensor_tensor", st[:, 4:5], st[:, 1:2], 1.0 / 512, st[:, 3:4], ALU.mult, ALU.subtract,
                reads=[st], writes=[st])
            P.v("vector", "tensor_scalar", st[:, 4:5], st[:, 4:5], 0.0, GN_EPS, ALU.max, ALU.add, reads=[st], writes=[st])
            P.act(st[:, 5:6], st[:, 4:5], AF.Sqrt, reads=[st], writes=[st])
            P.v("vector", "reciprocal", st[:, 6:7], st[:, 5:6], reads=[st], writes=[st])
            P.v("vector", "tensor_scalar", o[:], o[:], st[:, 2:3], st[:, 6:7], ALU.subtract, ALU.mult,
                reads=[o, st], writes=[o])
            P.v("vector", "tensor_tensor", o[:], o[:], gn[:, 0, :], ALU.mult, reads=[o, gn], writes=[o])
            P.v("vector", "tensor_tensor", o[:], o[:], gn[:, 1, :], ALU.add, reads=[o, gn], writes=[o])
            sg = bs["sg"]
            P.act(sg[:], g[:], AF.Silu, reads=[g], writes=[sg])
            P.v("vector", "tensor_tensor", ob[:], o[:], sg[:], ALU.mult, reads=[o, sg], writes=[ob])
            P.dma("sync", out[tok0:tok0 + C, :], ob[:], reads=[ob])
            it += 1
    return P.finish()


def ret_consts(h):
    f32 = np.float32
    C = RET_C
    log_g = np.log1p(-np.exp2(f32(-5.0) - f32(h)).astype(f32)).astype(f32)
    idx = np.arange(C, dtype=f32)
    diff = idx[:, None] - idx[None, :]
    decay = np.where(diff >= 0, np.exp(np.maximum(diff, 0.0) * log_g), 0.0).astype(f32)
    scale = f32(256 ** -0.5)
    decT = np.ascontiguousarray((decay * scale).T.astype(f32))
    xi = np.exp((idx + 1.0) * log_g).astype(f32)
    zeta = np.exp((C - 1.0 - idx) * log_g).astype(f32)
    cd = np.exp(f32(C) * log_g).astype(f32)
    xirow = np.ascontiguousarray(np.broadcast_to(xi[None, :], (128, 128))).astype(f32)
    scal = np.stack([zeta * scale, np.full(128, cd, f32)], axis=1).astype(f32)
    return decT, xirow, np.ascontiguousarray(scal)


def rope_tables():
    half = 128
    inv = (10000.0 ** (-np.arange(half, dtype=np.float32) / half)).astype(np.float32)
    ang = np.arange(S, dtype=np.float32)[None, :] * inv[:, None]
    return np.ascontiguousarray(np.stack([np.cos(ang), np.sin(ang)], axis=1).astype(np.float32))


NQT = S // 128
QSCALE = 128 ** -0.5
NEG = -30000.0


def build_nsa():
    P = Prog()
    qT_d = P.dram("qT", [512, S], BF16)
    kcT_d = P.dram("kcT", [128, S], BF16)
    vcT_d = P.dram("vcT", [128, S], BF16)
    ksT_d = P.dram("ksT", [128, S], BF16)
    vs_d = P.dram("vs", [S, 128], BF16)
    kwT_d = P.dram("kwT", [128, S], BF16)
    vw_d = P.dram("vw", [S, 128], BF16)
    g_d = P.dram("g", [S, 12], F32)
    peT_d = P.dram("peT", [2, 128, 32], F32)
    w1_d = P.dram("w1", [2, 32, 128, 128], F32)
    b1_d = P.dram("b1", [128, 2], F32)
    w2_d = P.dram("w2", [2, 128, 128], F32)
    bS_d = P.dram("biasS", [9, 128, 512], BF16)
    bW_d = P.dram("biasW", [5, 128, 512], BF16)
    bC_d = P.dram("biasC", [128, 4, 1024], BF16)
    F_d = P.dram("selF", [128, 256], F32)
    E_d = P.dram("expand", [128, 64, 128], BF16)
    id_d = P.dram("ident", [128, 128], BF16)
    out = P.dram("o", [S, 512], BF16, kind="ExternalOutput")

    ident = P.sb([128, 128], BF16)
    ksT = P.sb([128, S], BF16)
    kwT = P.sb([128, S], BF16)
    vs = P.sb([128, NQT, 129], BF16)
    vw = P.sb([128, NQT, 129], BF16)
    gsb = P.sb([128, NQT, 12], F32)
    bS = P.sb([128, 9, 512], BF16)
    bW = P.sb([128, 5, 512], BF16)
    bC = P.sb([128, 4, 1024], BF16)
    selF = P.sb([128, 256], F32)
    Eall = P.sb([128, 64, 128], BF16)
    kcmpT = P.sb([128, 512], BF16)
    vcmp = P.sb([128, 4, 128], BF16)
    P.dma("sync", ident[:], id_d[:, :], writes=[ident])
    for i in range(4):
        sl = slice(i * 2048, (i + 1) * 2048)
        P.dma("sync", ksT[:, sl], ksT_d[:, sl], writes=[ksT])
        P.dma("sync", kwT[:, sl], kwT_d[:, sl], writes=[kwT])
    P.v("vector", "memset", vs[:, :, 128:129], 1.0, writes=[vs])
    P.v("vector", "memset", vw[:, :, 128:129], 1.0, writes=[vw])
    for i in range(4):
        tl = slice(i * 16, (i + 1) * 16)
        rl = slice(i * 2048, (i + 1) * 2048)
        P.dma("sync", vs[:, tl, 0:128], vs_d[rl, :].rearrange("(t p) d -> p t d", p=128), writes=[vs])
        P.dma("sync", vw[:, tl, 0:128], vw_d[rl, :].rearrange("(t p) d -> p t d", p=128), writes=[vw])
        P.dma("sync", gsb[:, tl, :], g_d[rl, :].rearrange("(t p) c -> p t c", p=128), writes=[gsb])
    P.dma("sync", bS[:], bS_d.t.rearrange("d k c -> k d c"), writes=[bS])
    P.dma("sync", bW[:], bW_d.t.rearrange("d k c -> k d c"), writes=[bW])
    P.dma("sync", bC[:], bC_d[:, :, :], writes=[bC])
    P.dma("sync", selF[:], F_d[:, :], writes=[selF])
    P.dma("sync", Eall[:], E_d[:, :, :], writes=[Eall])
    P.act(gsb[:], gsb[:], AF.Sigmoid, reads=[gsb], writes=[gsb])

    sc_ps = [P.ps([128, 512]) for _ in range(2)]
    tr_ps = P.ps([128, 512], BF16)
    oc_ps = P.ps([128, 512])
    oa_ps = [P.ps([128, 129]) for _ in range(4)]
    sci = [0]

    def next_sc():
        sci[0] += 1
        return sc_ps[sci[0] % 2]

    P.push()
    w1b = P.sb([128, 2, 32, 128], BF16)
    w2b = P.sb([128, 2, 128], BF16)
    b1 = P.sb([128, 2], F32)
    peT = P.sb([128, 2, 32], F32)
    for j in range(2):
        P.dma("gpsimd", w1b[:, j, :, :], w1_d[j].rearrange("l d e -> d l e"), writes=[w1b])
        P.dma("gpsimd", w2b[:, j, :], w2_d[j], writes=[w2b])
        P.dma("sync", peT[:, j, :], peT_d[j], writes=[peT])
    P.dma("sync", b1[:], b1_d[:, :], writes=[b1])
    P.v("vector", "memset", kcmpT[:], 0.0, writes=[kcmpT])
    P.v("vector", "memset", vcmp[:], 0.0, writes=[vcmp])
    tok = P.sb([128, S], BF16)
    tokA = P.sb([128, S], BF16)
    tokB = P.sb([128, S], BF16)
    hid = P.sb([128, 512], BF16)
    for j, src in ((0, kcT_d), (1, vcT_d)):
        for i in range(4):
            sl = slice(i * 2048, (i + 1) * 2048)
            P.dma("sync", tok[:, sl], src[:, sl], writes=[tok])
        t3 = tok[:].rearrange("p (n r) -> p n r", r=16)
        P.v("vector", "tensor_tensor", tokA[:].rearrange("p (n r) -> p n r", r=16), t3,
            peT[:, j, 0:16].unsqueeze(1).to_broadcast([128, 512, 16]), ALU.add, reads=[tok, peT], writes=[tokA])
        P.v("vector", "tensor_tensor", tokB[:].rearrange("p (n r) -> p n r", r=16), t3,
            peT[:, j, 16:32].unsqueeze(1).to_broadcast([128, 512, 16]), ALU.add, reads=[tok, peT], writes=[tokB])
        ps = next_sc()
        A3 = tokA[:].rearrange("p (n r) -> p n r", r=16)
        B3 = tokB[:].rearrange("p (n r) -> p n r", r=16)
        for l in range(32):
            rhs = A3[:, 0:511, l] if l < 16 else B3[:, 1:512, l - 16]
            P.mm(ps[:, 0:511], w1b[:, j, l, :], rhs, l == 0, l == 31, reads=[w1b, tokA, tokB], writes=[ps])
        P.v("vector", "memset", hid[:], 0.0, writes=[hid])
        P.act(hid[:, 0:511], ps[:, 0:511], AF.Gelu_apprx_tanh, bias=b1[:, j:j + 1], reads=[ps, b1], writes=[hid])
        if j == 0:
            ps2 = next_sc()
            P.mm(ps2[:, 0:511], w2b[:, 0, :], hid[:, 0:511], True, True, reads=[w2b, hid], writes=[ps2])
            P.v("vector", "tensor_copy", kcmpT[:, 0:511], ps2[:, 0:511], reads=[ps2], writes=[kcmpT])
        else:
            ps2 = next_sc()
            for nch in range(4):
                m = 128 if nch < 3 else 127
                P.mm(ps2[:m, nch * 128:(nch + 1) * 128], hid[:, nch * 128:nch * 128 + m], w2b[:, 1, :], True, True,
                     reads=[w2b, hid], writes=[ps2])
            for nch in range(4):
                m = 128 if nch < 3 else 127
                P.v("vector", "tensor_copy", vcmp[:m, nch, :], ps2[:m, nch * 128:(nch + 1) * 128],
                    reads=[ps2], writes=[vcmp])
    P.pop()

    def bufset():
        d = {}
        d["q"] = P.sb([128, 4, 128], BF16)
        d["e"] = [P.sb([128, 512], F32) for _ in range(2)]
        d["pb"] = P.sb([128, 4, 512], BF16)
        d["pT"] = P.sb([128, 4, 128], BF16)
        d["rs"] = P.sb([128, 8], F32)
        d["Pp"] = P.sb([128, 516], F32)
        d["sel"] = P.sb([128, 128], F32)
        d["sel2"] = P.sb([128, 128], F32)
        d["m8"] = P.sb([128, 16], F32)
        d["nm"] = P.sb([128, 128], BF16)
        d["nmT4"] = P.sb([128, 4, 128], BF16)
        d["cf"] = P.sb([128, 3, 4], F32)
        d["den"] = P.sb([128, 2, 4], F32)
        d["acc"] = P.sb([128, 4, 128], F32)
        d["ob"] = P.sb([128, 512], BF16)
        return d
    sets = [bufset(), bufset()]
    for bs in sets:
        P.v("vector", "memset", bs["Pp"][:], 0.0, writes=[bs["Pp"]])
    pTs = [P.sb([128, 4, 128], BF16) for _ in range(3)]
    pti = [0]

    def attend(qt, q, ktiles, kT, vext, bias_of, mask_of, acc_scale_col, bs, first):
        n = len(ktiles)
        for ii, kt in enumerate(ktiles):
            ps = next_sc()
            q2 = q[:].rearrange("p h t -> p (h t)")
            P.mm(ps[:, :], kT[:, kt * 128:(kt + 1) * 128], q2, True, False, reads=[kT, q], writes=[ps])
            if mask_of is not None:
                P.mm(ps[:, :], Eall[:, kt, :], bs["nmT4"][:].rearrange("p h t -> p (h t)"), False, False,
                     reads=[Eall, bs["nmT4"]], writes=[ps])
            bt, bi = bias_of(qt - kt)
            P.mm(ps[:, :], ident[:], bt[:, bi, :], False, True, reads=[ident, bt], writes=[ps])
            pti[0] += 1
            pT = pTs[pti[0] % 3]
            P.act(pT[:].rearrange("p h t -> p (h t)"), ps[:, :], AF.Exp, scale=QSCALE, reads=[ps], writes=[pT])
            for h in range(4):
                oa = oa_ps[h]
                P.mm(oa[:, :], pT[:, h, :], vext[:, kt, :], ii == 0, ii == n - 1, reads=[pT, vext], writes=[oa])

    it = 0
    for qt in range(NQT):
        bs = sets[it % 2]
        it += 1
        t0 = qt * 128
        q = bs["q"]
        P.dma("sync", q[:], qT_d[:, t0:t0 + 128].rearrange("(h p) t -> p h t", p=128), writes=[q])
        off = 504 - t0 // 16
        rs, Pp, pb = bs["rs"], bs["Pp"], bs["pb"]
        for h in range(4):
            ps = next_sc()
            P.mm(ps[:, :], q[:, h, :], kcmpT[:], True, False, reads=[q, kcmpT], writes=[ps])
            P.mm(ps[:, :], ident[:], bC[:, h, off:off + 512], False, True, reads=[ident, bC], writes=[ps])
            e = bs["e"][h % 2]
            P.act(e[:], ps[:, :], AF.Exp, scale=QSCALE, reads=[ps], writes=[e])
            P.v("vector", "reduce_sum", rs[:, h:h + 1], e[:], AX.X, reads=[e], writes=[rs])
            P.v("vector", "tensor_scalar_max", rs[:, h:h + 1], rs[:, h:h + 1], 1e-30, reads=[rs], writes=[rs])
            P.v("vector", "reciprocal", rs[:, 4 + h:5 + h], rs[:, h:h + 1], reads=[rs], writes=[rs])
            P.act(pb[:, h, :], e[:], AF.Copy, scale=rs[:, 4 + h:5 + h], reads=[e, rs], writes=[pb])
            if h == 0:
                P.v("vector", "tensor_scalar_mul", Pp[:, 1:513], e[:], rs[:, 4:5], reads=[e, rs], writes=[Pp])
            else:
                P.v("vector", "scalar_tensor_tensor", Pp[:, 1:513], e[:], rs[:, 4 + h:5 + h], Pp[:, 1:513],
                    ALU.mult, ALU.add, reads=[e, rs, Pp], writes=[Pp])
        for h in range(4):
            for nch in range(4):
                P.op("tensor", lambda e_, o_=tr_ps[:, nch * 128:(nch + 1) * 128], i_=pb[:, h, nch * 128:(nch + 1) * 128]:
                     e_.transpose(o_, i_, ident[:]), reads=[pb, ident], writes=[tr_ps])
            pT = bs["pT"]
            P.v("vector", "tensor_copy", pT[:].rearrange("p c t -> p (c t)"), tr_ps[:, :], reads=[tr_ps], writes=[pT])
            for nch in range(4):
                P.mm(oc_ps[:, h * 128:(h + 1) * 128], pT[:, nch, :], vcmp[:, nch, :], nch == 0, nch == 3,
                     reads=[pT, vcmp], writes=[oc_ps])
        sel, sel2, m8, nm = bs["sel"], bs["sel2"], bs["m8"], bs["nm"]
        P.v("vector", "tensor_tensor", sel[:], Pp[:, 0:512:4], Pp[:, 1:513:4], ALU.add, reads=[Pp], writes=[sel])
        for r_ in range(2, 5):
            P.v("vector", "tensor_tensor", sel[:], sel[:], Pp[:, r_:r_ + 512:4], ALU.add, reads=[Pp, sel], writes=[sel])
        P.v("vector", "tensor_tensor", sel[:], sel[:], selF[:, 128 - 2 * qt:256 - 2 * qt], ALU.add,
            reads=[sel, selF], writes=[sel])
        P.v("vector", "tensor_scalar_add", sel[:, 0:1], sel[:, 0:1], 1e6, reads=[sel], writes=[sel])
        P.v("vector", "max", out=m8[:, 0:8], in_=sel[:], reads=[sel], writes=[m8])
        P.v("vector", "match_replace", out=sel2[:], in_to_replace=m8[:, 0:8], in_values=sel[:], imm_value=-3e38,
            reads=[sel, m8], writes=[sel2])
        P.v("vector", "max", out=m8[:, 8:16], in_=sel2[:], reads=[sel2], writes=[m8])
        P.v("vector", "tensor_scalar", nm[:], sel[:], m8[:, 15:16], NEG / QSCALE, ALU.is_lt, ALU.mult,
            reads=[sel, m8], writes=[nm])
        P.op("tensor", lambda e_, o_=tr_ps[:, 0:128], i_=nm[:]: e_.transpose(o_, i_, ident[:]),
             reads=[nm, ident], writes=[tr_ps])
        P.v("vector", "tensor_copy", bs["nmT4"][:], tr_ps[:, 0:128].unsqueeze(1).to_broadcast([128, 4, 128]),
            reads=[tr_ps], writes=[bs["nmT4"]])
        cf, den, acc = bs["cf"], bs["den"], bs["acc"]
        for h in range(4):
            P.v("vector", "tensor_scalar_mul", acc[:, h, :], oc_ps[:, h * 128:(h + 1) * 128], gsb[:, qt, h:h + 1],
                reads=[oc_ps, gsb], writes=[acc])
        for br, (ktiles, kT, vext, bias_of, use_mask) in enumerate((
                ([kt for kt in range(max(0, qt - 4), qt + 1)], kwT, vw, (lambda dl: (bW, dl)), None),
                ([kt for kt in range(0, qt + 1)], ksT, vs, (lambda dl: (bS, min(dl, 8))), True))):
            attend(qt, q, ktiles, kT, vext, bias_of, use_mask, None, bs, True)
            gcol = 8 if br == 0 else 4
            for h in range(4):
                P.v("vector", "tensor_scalar_max", den[:, br, h:h + 1], oa_ps[h][:, 128:129], 1e-30,
                    reads=[oa_ps[h]], writes=[den])
            P.v("vector", "reciprocal", den[:, br, :], den[:, br, :], reads=[den], writes=[den])
            P.v("vector", "tensor_tensor", cf[:, br, :], den[:, br, :], gsb[:, qt, gcol:gcol + 4], ALU.mult,
                reads=[den, gsb], writes=[cf])
            for h in range(4):
                P.v("vector", "scalar_tensor_tensor", acc[:, h, :], oa_ps[h][:, 0:128], cf[:, br, h:h + 1],
                    acc[:, h, :], ALU.mult, ALU.add, reads=[oa_ps[h], cf, acc], writes=[acc])
        ob = bs["ob"]
        P.v("scalar", "copy", ob[:], acc[:].rearrange("p h d -> p (h d)"), reads=[acc], writes=[ob])
        P.dma("sync", out[t0:t0 + 128, :], ob[:], reads=[ob])
    return P.finish()


def t5_bucket_np(n):
    n = np.maximum(n, 0)
    max_exact = 16
    lr = np.log(np.maximum(n, 1).astype(np.float32) / np.float32(max_exact)) / np.float32(math.log(1024 / max_exact))
    large = np.minimum(max_exact + (lr.astype(np.float32) * np.float32(32 - max_exact)).astype(np.int32), 31)
    return np.where(n < max_exact, n, large)


def nsa_consts(rel_bias, g):
    inv = np.float32(1.0 / QSCALE)
    ftab = rel_bias[t5_bucket_np(np.arange(S)), 4 * g:4 * g + 4].T.astype(np.float32) * inv
    k = np.arange(128)[:, None]
    q = np.arange(128)[None, :]
    bS = np.empty((9, 128, 4, 128), np.float32)
    for dl in range(9):
        dist = 128 * dl + q - k
        ok = dist >= 0
        dd = np.clip(dist, 0, S - 1)
        for h in range(4):
            bS[dl, :, h, :] = np.where(ok, ftab[h][dd], NEG * inv)
    bW = np.empty((5, 128, 4, 128), np.float32)
    for dl in range(5):
        dist = 128 * dl + q - k
        ok = (dist >= 0) & (dist < 512)
        dd = np.clip(dist, 0, S - 1)
        for h in range(4):
            bW[dl, :, h, :] = np.where(ok, ftab[h][dd], NEG * inv)
    ql = np.arange(128)[:, None]
    j = np.arange(1024)[None, :]
    dist = ql - 16 * (j - 504) - 31
    ok = dist >= 0
    dd = np.clip(dist, 0, S - 1)
    bC = np.empty((128, 4, 1024), np.float32)
    for h in range(4):
        bC[:, h, :] = np.where(ok, ftab[h][dd], NEG * inv)
    bp = np.arange(256)[None, :] - 128
    cur = (np.arange(128)[:, None] >= 64).astype(np.int64)
    F = np.where((bp == cur) | (bp == cur - 1), 1e6, np.where(bp > cur, -1e6, 0.0)).astype(np.float32)
    E = np.zeros((128, 64, 128), np.float32)
    for kt in range(64):
        E[2 * kt, kt, 0:64] = 1.0
        E[2 * kt + 1, kt, 64:128] = 1.0
    return (bS.reshape(9, 128, 512).astype(NPBF), bW.reshape(5, 128, 512).astype(NPBF), bC.astype(NPBF),
            F, E.astype(NPBF))


def nsa_in_maps(inT, rel_bias, cmp_pe, cmp_w1, cmp_b1, cmp_w2):
    ident = np.eye(128, dtype=np.float32).astype(NPBF)
    peT = np.ascontiguousarray(cmp_pe.transpose(0, 2, 1))
    b1 = np.ascontiguousarray(cmp_b1.T)
    maps = []
    for c in range(NCORE):
        b, g = c // 4, c % 4
        ts = slice(b * S, (b + 1) * S)
        bS, bW, bC, F, E = nsa_consts(rel_bias, g)

        def kv(i, T=False):
            r0 = 2048 + i * 512 + g * 128
            a = inT[r0:r0 + 128, ts]
            return np.ascontiguousarray(a.T if T else a)
        gi = 5120 + np.arange(3)[:, None] * 16 + 4 * g + np.arange(4)[None, :]
        maps.append({"qT": np.ascontiguousarray(inT[g * 512:(g + 1) * 512, ts]),
                     "kcT": kv(0), "vcT": kv(1), "ksT": kv(2), "vs": kv(3, True), "kwT": kv(4), "vw": kv(5, True),
                     "g": np.ascontiguousarray(inT[gi.reshape(-1), ts].T.astype(np.float32)),
                     "peT": peT, "w1": cmp_w1, "b1": b1, "w2": cmp_w2,
                     "biasS": bS, "biasW": bW, "biasC": bC, "selF": F, "expand": E, "ident": ident})
    return maps


_PROGS = {}


def _prog(key, builder, *args):
    if key not in _PROGS:
        _PROGS[key] = builder(*args)
    return _PROGS[key]


def _run(nc, in_maps):
    return run_bass_kernel_spmd(nc, in_maps, core_ids=list(range(NCORE))).results


def kernel(x, c, rel_bias, ada_w, ada_b, ln_w, ln_b, mlp_w1, mlp_w2,
           lru_w_in, lru_conv_w, lru_conv_b, lru_gate_w, lru_gate_b, lru_lambda, lru_w_out,
           nsa_w_in, nsa_cmp_pe, nsa_cmp_w1, nsa_cmp_b1, nsa_cmp_w2, nsa_w_out,
           ret_w_in, ret_gn_w, ret_gn_b, ret_w_out):
    f32 = np.float32
    A = lambda a: np.ascontiguousarray(np.asarray(a))
    x = np.asarray(x, f32)
    c = np.asarray(c, f32)
    cT = A(c.T.reshape(16, 128, 2).transpose(1, 0, 2))
    maps = []
    for i in range(NCORE):
        l, j = i // 2, i % 2
        maps.append({"cT": cT, "W": A(ada_w[l, j]), "bias": A(np.broadcast_to(np.asarray(ada_b[l, j]), (2, 3 * D)))})
    res = _run(_prog("p0", build_p0), maps)
    mod = np.stack([r["mod"] for r in res]).reshape(DEPTH, 2, 2, 3 * D).transpose(0, 2, 1, 3)
    xT = A(x.reshape(NTOK, D).T)
    ident = np.eye(128, dtype=f32).astype(NPBF)
    for layer in range(DEPTH):
        mixer, inst = layer % 3, layer // 3
        if mixer == 0:
            W_in, N, odt = lru_w_in[inst], 4096, F32
        elif mixer == 1:
            W_in, N, odt = nsa_w_in[inst], 5168, BF16
        else:
            W_in, N, odt = ret_w_in[inst], 12288, BF16
        W_in = A(W_in)
        maps = []
        for ci in range(NCORE):
            b = ci // 4
            sh, sc, _g = np.split(mod[layer, b, 0], 3)
            maps.append({"xT": A(xT[:, ci * TPC:(ci + 1) * TPC]), "vec": A(np.stack([vec16(sc), vec16(sh)], axis=1)),
                         "W": W_in})
        res = _run(_prog(("p1", N), build_p1, N, odt), maps)
        inT = np.concatenate([r["outT"] for r in res], axis=1)
        del res
        if mixer == 0:
            maps = []
            for j in range(NCORE):
                ch = slice(j * 256, (j + 1) * 256)
                vs_ = np.stack([lru_conv_w[inst][0, ch], lru_conv_w[inst][1, ch], lru_conv_w[inst][2, ch],
                                lru_conv_w[inst][3, ch], lru_conv_b[inst][ch], lru_gate_b[inst][0, ch],
                                lru_gate_b[inst][1, ch], lru_lambda[inst][ch]], axis=1).astype(f32)
                maps.append({"xbr": A(inT[2048 + j * 256:2048 + (j + 1) * 256]), "ybr": A(inT[j * 256:(j + 1) * 256]),
                             "cvec": A(vs_.reshape(2, 128, 8).transpose(1, 0, 2)), "gw": A(lru_gate_w[inst][:, j])})
            res = _run(_prog("lru", build_lru), maps)
            mixT = np.concatenate([r["mixT"] for r in res], axis=0)
            Wo, KIN = lru_w_out[inst], 2048
        elif mixer == 1:
            maps = nsa_in_maps(inT, np.asarray(rel_bias, f32), np.asarray(nsa_cmp_pe[inst], f32),
                               A(nsa_cmp_w1[inst]), np.asarray(nsa_cmp_b1[inst], f32), A(nsa_cmp_w2[inst]))
            res = _run(_prog("nsa", build_nsa), maps)
            mixT = np.empty((2048, NTOK), NPBF)
            for ci in range(NCORE):
                b, g = ci // 4, ci % 4
                mixT[g * 512:(g + 1) * 512, b * S:(b + 1) * S] = res[ci]["o"].T
            Wo, KIN = nsa_w_out[inst], 2048
        else:
            cs = rope_tables()
            maps = []
            for h in range(NCORE):
                decT, xirow, scal = ret_consts(h)
                gn = np.stack([np.broadcast_to(np.asarray(ret_gn_w[inst])[h * 512:(h + 1) * 512], (128, 512)),
                               np.broadcast_to(np.asarray(ret_gn_b[inst])[h * 512:(h + 1) * 512], (128, 512))],
                              axis=1).astype(f32)
                maps.append({"qT": A(inT[h * 256:(h + 1) * 256]), "kT": A(inT[2048 + h * 256:2048 + (h + 1) * 256]),
                             "v": A(inT[4096 + h * 512:4096 + (h + 1) * 512].T),
                             "g": A(inT[8192 + h * 512:8192 + (h + 1) * 512].T),
                             "cs": cs, "decT": decT, "xirow": xirow, "scal": scal, "gn": A(gn), "ident": ident})
            res = _run(_prog("ret", build_ret), maps)
            mixT = np.empty((4096, NTOK), NPBF)
            for h in range(NCORE):
                mixT[h * 512:(h + 1) * 512, :] = res[h]["o"].T
            Wo, KIN = ret_w_out[inst], 4096
        del res, inT
        Wo = A(Wo)
        W1 = A(mlp_w1[layer])
        W2l = w2_layout(np.asarray(mlp_w2[layer]))
        maps = []
        for ci in range(NCORE):
            b = ci // 4
            _s1, _c1, g1 = np.split(mod[layer, b, 0], 3)
            sh2, sc2, g2 = np.split(mod[layer, b, 1], 3)
            vec = A(np.stack([vec16(g1), vec16(sc2), vec16(sh2), vec16(g2), vec16(ln_w[layer, 0]), vec16(ln_b[layer, 0]),
                              vec16(ln_w[layer, 1]), vec16(ln_b[layer, 1])], axis=1))
            tsl = slice(ci * TPC, (ci + 1) * TPC)
            maps.append({"mixT": A(mixT[:, tsl]), "Wo": Wo, "xT": A(xT[:, tsl]), "vec": vec, "W1": W1, "W2": W2l})
        res = _run(_prog(("p2", KIN), build_p2, KIN, BF16), maps)
        xT = np.concatenate([r["x2T"] for r in res], axis=1)
        del res, mixT
    return A(xT.T).reshape(B, S, D).astype(f32)
```
